# Optimizing a Trainium2 kernel written in Bass

```python
import math
import jax, jax.numpy as jnp
from jax import lax
import numpy as np

D_MODEL = 2048
BATCH = 4
SEQ = 8192
DEPTH = 4

CHUNK = 64
Q_BLOCK = 128
N_MIXERS = 3
FFN_DIM = 5632
ATTN_HEADS = 16
ATTN_QK_DIM = D_MODEL // ATTN_HEADS // 2
ATTN_V_DIM = 2 * ATTN_QK_DIM
CONV_WIDTH = 3
HGRN_HEADS = 16
HGRN_HEAD_DIM = D_MODEL // HGRN_HEADS
N_SUBLAYERS = 3
N_MOD = 3 * N_SUBLAYERS
N_ATTN_LAYERS = (DEPTH + 2) // 3
N_CONV_LAYERS = (DEPTH + 1) // 3
N_HGRN_LAYERS = DEPTH // 3
RMS_EPS = 1e-6

kernel_name = "hybrid_diffattn_shortconv_hgrn2_macaron"


def rms_norm(x, g):
    xf = x.astype(jnp.float32)
    y = xf * lax.rsqrt(jnp.mean(xf * xf, axis=-1, keepdims=True) + RMS_EPS)
    return (y * g.astype(jnp.float32)).astype(x.dtype)


def modulate(h, shift, scale):
    return h * (1 + scale[:, None, :]) + shift[:, None, :]


def swiglu(h, w_gate, w_up, w_down):
    return (jax.nn.silu(h @ w_gate) * (h @ w_up)) @ w_down


def alibi_slopes(n_heads):
    return jnp.asarray(np.array([2.0 ** (-8.0 * (h + 1) / n_heads) for h in range(n_heads)], dtype=np.float32))


def diff_attention(h, w_in, w_out, q_gain, k_gain, lam_vecs, subln_g, lambda_init):
    B, S, _ = h.shape
    H, dh, dv = ATTN_HEADS, ATTN_QK_DIM, ATTN_V_DIM
    q, k, v = jnp.split(h @ w_in, 3, axis=-1)
    q = rms_norm(q.reshape(B, S, H, 2, dh), q_gain)
    k = rms_norm(k.reshape(B, S, H, 2, dh), k_gain)
    v = v.reshape(B, S, H, dv)
    lam = (jnp.exp(jnp.sum(lam_vecs[0].astype(jnp.float32) * lam_vecs[1].astype(jnp.float32)))
           - jnp.exp(jnp.sum(lam_vecs[2].astype(jnp.float32) * lam_vecs[3].astype(jnp.float32)))
           + lambda_init)
    slopes = alibi_slopes(H)
    k_pos = jnp.arange(S)
    n_blk = S // Q_BLOCK
    q_blocks = q.reshape(B, n_blk, Q_BLOCK, H, 2, dh).transpose(1, 0, 2, 3, 4, 5)
    scale = dh ** -0.5

    def attend(args):
        q_blk, blk = args
        q_pos = blk * Q_BLOCK + jnp.arange(Q_BLOCK)
        s = jnp.einsum('bqhjd,bkhjd->bhjqk', q_blk, k, preferred_element_type=jnp.float32) * scale
        dist = jnp.abs(q_pos[:, None] - k_pos[None, :]).astype(jnp.float32)
        s = s - (slopes[:, None, None] * dist)[None, :, None]
        mask = (k_pos[None, :] // CHUNK) <= (q_pos[:, None] // CHUNK)
        p = jax.nn.softmax(jnp.where(mask, s, -jnp.inf), axis=-1)
        a = p[:, :, 0] - lam * p[:, :, 1]
        return jnp.einsum('bhqk,bkhv->bqhv', a.astype(v.dtype), v)

    o = lax.map(attend, (q_blocks, jnp.arange(n_blk)))
    o = o.transpose(1, 0, 2, 3, 4).reshape(B, S, H, dv)
    o = rms_norm(o, subln_g) * (1.0 - lambda_init)
    return o.reshape(B, S, H * dv) @ w_out


def short_conv_mixer(h, w_in, conv_w, w_out):
    b_gate, c_gate, u = jnp.split(h @ w_in, 3, axis=-1)
    v = c_gate * u
    y = lax.conv_general_dilated(
        v, conv_w[:, None, :].astype(v.dtype), window_strides=(1,),
        padding=[(CONV_WIDTH - 1, 0)], dimension_numbers=('NWC', 'WIO', 'NWC'),
        feature_group_count=D_MODEL)
    return (b_gate * y) @ w_out


def hgrn2_mixer(h, w_in, w_out, o_norm_g, lower_bound):
    B, S, _ = h.shape
    H, dk = HGRN_HEADS, HGRN_HEAD_DIM
    q, f_logit, i, g = jnp.split(h @ w_in, 4, axis=-1)
    lb = lower_bound.astype(jnp.float32).reshape(H, dk)
    f = lb + (1.0 - lb) * jax.nn.sigmoid(f_logit.reshape(B, S, H, dk).astype(jnp.float32))
    log_f = jnp.log(f)
    k = 1.0 - f
    n_chunks = S // CHUNK

    def to_chunks(t):
        return t.reshape(B, n_chunks, CHUNK, H, dk).transpose(1, 0, 3, 2, 4).astype(jnp.float32)

    causal = jnp.tril(jnp.ones((CHUNK, CHUNK), dtype=bool))

    def step(state, xs):
        q_c, k_c, i_c, lf_c = xs
        b = jnp.cumsum(lf_c, axis=-2)
        inter = jnp.einsum('bhtk,bhkv->bhtv', q_c * jnp.exp(b), state)
        diff = b[:, :, :, None, :] - b[:, :, None, :, :]
        decay = jnp.exp(jnp.where(causal[:, :, None], diff, -jnp.inf))
        scores = jnp.einsum('bhtsk,bhsk->bhts', q_c[:, :, :, None, :] * decay, k_c)
        intra = jnp.einsum('bhts,bhsv->bhtv', scores, i_c)
        b_last = b[:, :, -1:, :]
        state = (jnp.exp(b_last[:, :, 0, :, None]) * state
                 + jnp.einsum('bhsk,bhsv->bhkv', k_c * jnp.exp(b_last - b), i_c))
        return state, inter + intra

    state0 = jnp.zeros((B, H, dk, dk), jnp.float32)
    _, o = lax.scan(step, state0, (to_chunks(q), to_chunks(k), to_chunks(i), to_chunks(log_f)))
    o = o.transpose(1, 0, 3, 2, 4).reshape(B, S, H, dk)
    o = rms_norm(o, o_norm_g) * jax.nn.silu(g.reshape(B, S, H, dk).astype(jnp.float32))
    return o.reshape(B, S, H * dk).astype(h.dtype) @ w_out


def setup_inputs(seed: int = 0) -> dict:
    key = jax.random.key(seed)
    ks = jax.random.split(key, 24)
    D, F = D_MODEL, FFN_DIM

    def nrm(k, shape, scale):
        return jax.random.normal(k, shape, jnp.float32) * scale

    return {
        "x": nrm(ks[0], (BATCH, SEQ, D), 1.0),
        "c": nrm(ks[1], (BATCH, D), 1.0),
        "ada_w": nrm(ks[2], (DEPTH, D, N_MOD * D), 0.5 * D ** -0.5),
        "ada_b": nrm(ks[3], (DEPTH, N_MOD * D), 0.02),
        "norm_g": 1.0 + nrm(ks[4], (DEPTH, N_SUBLAYERS, D), 0.02),
        "ffn_w_gate": nrm(ks[5], (DEPTH, 2, D, F), D ** -0.5),
        "ffn_w_up": nrm(ks[6], (DEPTH, 2, D, F), D ** -0.5),
        "ffn_w_down": nrm(ks[7], (DEPTH, 2, F, D), F ** -0.5),
        "attn_w_in": nrm(ks[8], (N_ATTN_LAYERS, D, 3 * D), D ** -0.5),
        "attn_w_out": nrm(ks[9], (N_ATTN_LAYERS, D, D), D ** -0.5),
        "attn_q_gain": 1.0 + nrm(ks[10], (N_ATTN_LAYERS, ATTN_QK_DIM), 0.02),
        "attn_k_gain": 1.0 + nrm(ks[11], (N_ATTN_LAYERS, ATTN_QK_DIM), 0.02),
        "attn_lambda": nrm(ks[12], (N_ATTN_LAYERS, 4, ATTN_QK_DIM), 0.1),
        "attn_subln_g": 1.0 + nrm(ks[13], (N_ATTN_LAYERS, ATTN_V_DIM), 0.02),
        "conv_w_in": nrm(ks[14], (N_CONV_LAYERS, D, 3 * D), D ** -0.5),
        "conv_w": nrm(ks[15], (N_CONV_LAYERS, CONV_WIDTH, D), CONV_WIDTH ** -0.5),
        "conv_w_out": nrm(ks[16], (N_CONV_LAYERS, D, D), D ** -0.5),
        "hgrn_w_in": nrm(ks[17], (N_HGRN_LAYERS, D, 4 * D), D ** -0.5),
        "hgrn_w_out": nrm(ks[18], (N_HGRN_LAYERS, D, D), D ** -0.5),
        "hgrn_o_norm_g": 1.0 + nrm(ks[19], (N_HGRN_LAYERS, HGRN_HEAD_DIM), 0.02),
        "hgrn_lb_logits": nrm(ks[20], (DEPTH, D), 0.1),
    }


def reference(x, c, ada_w, ada_b, norm_g, ffn_w_gate, ffn_w_up, ffn_w_down,
              attn_w_in, attn_w_out, attn_q_gain, attn_k_gain, attn_lambda, attn_subln_g,
              conv_w_in, conv_w, conv_w_out,
              hgrn_w_in, hgrn_w_out, hgrn_o_norm_g, hgrn_lb_logits):
    cond = jax.nn.silu(c)
    lb_cum = jnp.cumsum(jax.nn.softmax(hgrn_lb_logits.astype(jnp.float32), axis=0), axis=0)
    lb_all = lb_cum - lb_cum[0:1]
    for layer in range(DEPTH):
        mod = (cond @ ada_w[layer] + ada_b[layer]).reshape(-1, N_MOD, D_MODEL)
        sh1, sc1, g1, sh2, sc2, g2, sh3, sc3, g3 = [mod[:, j] for j in range(N_MOD)]
        h = modulate(rms_norm(x, norm_g[layer, 0]), sh1, sc1)
        x = x + 0.5 * g1[:, None, :] * swiglu(h, ffn_w_gate[layer, 0], ffn_w_up[layer, 0], ffn_w_down[layer, 0])
        h = modulate(rms_norm(x, norm_g[layer, 1]), sh2, sc2)
        kind, slot = layer % N_MIXERS, layer // N_MIXERS
        if kind == 0:
            y = diff_attention(h, attn_w_in[slot], attn_w_out[slot], attn_q_gain[slot], attn_k_gain[slot],
                               attn_lambda[slot], attn_subln_g[slot], 0.8 - 0.6 * math.exp(-0.3 * layer))
        elif kind == 1:
            y = short_conv_mixer(h, conv_w_in[slot], conv_w[slot], conv_w_out[slot])
        else:
            y = hgrn2_mixer(h, hgrn_w_in[slot], hgrn_w_out[slot], hgrn_o_norm_g[slot], lb_all[layer])
        x = x + g2[:, None, :] * y
        h = modulate(rms_norm(x, norm_g[layer, 2]), sh3, sc3)
        x = x + 0.5 * g3[:, None, :] * swiglu(h, ffn_w_gate[layer, 1], ffn_w_up[layer, 1], ffn_w_down[layer, 1])
    return x
```

```python
import math
import numpy as np
import concourse.bass as bass
import concourse.mybir as mybir

F32 = mybir.dt.float32
BF16 = mybir.dt.bfloat16
AF = mybir.ActivationFunctionType
ALU = mybir.AluOpType
P = 128
RMS_EPS = 1e-6
CHUNK = 64
STRICT = True


class Op:
    __slots__ = ("eng", "fn", "deps", "signal", "count", "dma", "dsem", "dval", "prewait")

    def __init__(self, eng, fn, dma):
        self.eng = eng; self.fn = fn; self.deps = []; self.signal = False
        self.count = 0; self.dma = dma; self.dsem = None; self.dval = 0; self.prewait = None


class Prog:
    ENGS = ("pe", "act", "dve", "pool", "sp")
    QUEUES = ("sp", "pool", "act")
    NDS = 8

    def __init__(self, nc):
        self.nc = nc
        self.ops = {e: [] for e in self.ENGS}
        self.last_w = {}
        self.readers = {}
        self.region_of = {}
        self.hazards = {}
        self.esem = {e: nc.alloc_semaphore(name=f"prog_{e}") for e in self.ENGS}
        self.dsems = {q: [nc.alloc_semaphore(name=f"dma_{q}_{i}") for i in range(self.NDS)] for q in self.QUEUES}
        self.ndma = {q: 0 for q in self.QUEUES}

    def _haz(self, k, eng, dma, deps):
        if k in self.last_w or k in self.readers:
            return
        reg = self.region_of.get(k[0] if isinstance(k, tuple) else k)
        if reg is None:
            return
        for h in self.hazards.get(reg, ()):
            if dma or h.dma or h.eng != eng or (STRICT and eng != "pe"):
                deps.append(h)

    def add(self, eng, fn, reads=(), writes=(), dma=False):
        op = Op(eng, fn, dma)
        deps = op.deps
        lw = self.last_w; rd = self.readers
        for k in reads:
            self._haz(k, eng, dma, deps)
            w = lw.get(k)
            if w is not None and not (eng == "pe" and w.eng == "pe"):
                deps.append(w)
        for k in writes:
            self._haz(k, eng, dma, deps)
            w = lw.get(k)
            if w is not None and (dma or w.dma or w.eng != eng or (STRICT and eng != "pe")):
                deps.append(w)
            for r in rd.get(k, ()):
                if dma or r.dma or r.eng != eng or (STRICT and eng != "pe"):
                    deps.append(r)
        for d in deps:
            if not d.dma:
                d.signal = True
        for k in reads:
            lst = rd.setdefault(k, [])
            if not dma:
                for i, r in enumerate(lst):
                    if (not r.dma) and r.eng == eng:
                        lst[i] = op
                        break
                else:
                    lst.append(op)
            else:
                lst.append(op)
        for k in writes:
            lw[k] = op
            rd[k] = []
        if dma:
            q = eng
            n = self.ndma[q]; self.ndma[q] = n + 1
            op.dsem = self.dsems[q][n % self.NDS]
            op.dval = 16 * (n // self.NDS + 1)
            if n >= self.NDS:
                op.prewait = (op.dsem, 16 * (n // self.NDS))
        self.ops[eng].append(op)
        return op

    def end_phase(self, region):
        haz = list(self.hazards.get(region, ()))
        for k in list(self.last_w):
            if self.region_of.get(k[0] if isinstance(k, tuple) else k) == region:
                haz.append(self.last_w.pop(k))
        for k in list(self.readers):
            if self.region_of.get(k[0] if isinstance(k, tuple) else k) == region:
                haz.extend(self.readers.pop(k))
        best = {}
        out = []
        seen = set()
        order = {e: {id(o): i for i, o in enumerate(self.ops[e])} for e in self.ENGS} if False else None
        for h in haz:
            if id(h) in seen:
                continue
            seen.add(id(h))
            if h.dma:
                out.append(h)
            else:
                b = best.get(h.eng)
                if b is None or self._idx(h) > self._idx(b):
                    best[h.eng] = h
        out.extend(best.values())
        self.hazards[region] = out

    def _idx(self, op):
        c = getattr(self, "_idx_cache", None)
        if c is None:
            c = self._idx_cache = {}
        v = c.get(id(op))
        if v is None:
            lst = self.ops[op.eng]
            for i in range(len(lst) - 1, -1, -1):
                if lst[i] is op:
                    v = i
                    break
            c[id(op)] = v
        return v

    def emit(self):
        nc = self.nc
        for e in self.ENGS:
            c = 0
            for op in self.ops[e]:
                if op.signal and not op.dma:
                    c += 1
                    op.count = c
        engobj = {"pe": "tensor", "act": "scalar", "dve": "vector", "pool": "gpsimd", "sp": "sync"}
        outer = self

        def run(e, eng):
            waited = {}
            sem_e = outer.esem[e]
            for op in outer.ops[e]:
                need = {}
                if op.prewait is not None:
                    need[op.prewait[0]] = op.prewait[1]
                for d in op.deps:
                    if d.dma:
                        s, v = d.dsem, d.dval
                    else:
                        s, v = outer.esem[d.eng], d.count
                    if need.get(s, 0) < v:
                        need[s] = v
                for s, v in need.items():
                    if waited.get(s, 0) < v:
                        eng.wait_ge(s, v)
                        waited[s] = v
                ins = op.fn(eng)
                if op.dma:
                    ins.then_inc(op.dsem, 16)
                elif op.signal:
                    ins.then_inc(sem_e, 1)
            if e == "sp":
                for q in outer.QUEUES:
                    n = outer.ndma[q]
                    for i in range(outer.NDS):
                        cnt = len(range(i, n, outer.NDS))
                        if cnt > 0:
                            eng.wait_ge(outer.dsems[q][i], 16 * cnt)

        with nc.Block() as block:
            for e in self.ENGS:
                getattr(block, engobj[e])(lambda eng, e=e: run(e, eng))


class Cfg:
    def __init__(self, **kw):
        self.D = 2048; self.F = 5632; self.NT = 4096; self.T = 1024
        self.depth = 4
        self.NPFX = 0
        self.kinds = [0, 1, 2, 0]
        self.slopes = None
        self.__dict__.update(kw)
        self.DC = self.D // P
        self.FC = self.F // P
        self.H = self.D // P
        parts = []; c = 0
        while c < self.FC:
            n = min(24, self.FC - c); parts.append((c, n)); c += n
        self.fparts = parts
        if self.slopes is None:
            self.slopes = [2.0 ** (-8.0 * (h + 1) / self.H) for h in range(self.H)]
        self.NKEYS = self.NPFX + self.NT
        self.QW = []
        self.WIN = []
        for sl in self.slopes:
            qw = 128
            for cand in (256, 512):
                if sl * cand / 2 <= 45.5 and cand <= self.NT:
                    qw = cand
            self.QW.append(qw)
            self.WIN.append(int(math.ceil(80.0 / sl)))
        self.NREL_NEG = self.NKEYS // P
        self.NR = self.NREL_NEG + 4

    def lambda_init(self, layer):
        return 0.8 - 0.6 * math.exp(-0.3 * layer)


def host_tables(cfg):
    H = cfg.H
    NR = cfg.NR
    bias = np.zeros((P, H, NR), np.float32)
    mt = np.zeros((H, P, 4, 512), np.float32)
    p = np.arange(P)
    for h in range(H):
        sl = cfg.slopes[h]; qw = cfg.QW[h]
        for r in range(NR):
            rel = (r - cfg.NREL_NEG) * P
            bias[:, h, r] = sl * (p + rel - qw / 2)
        for m in range(qw // P):
            kpos = m * P + p[:, None]
            qpos = np.arange(qw)[None, :]
            ck = kpos // CHUNK; cq = qpos // CHUNK
            val = np.where(ck > cq, 0.0, np.where(kpos > qpos, np.exp(-2.0 * sl * (kpos - qpos)), 1.0))
            val = np.where(val < 1e-30, 0.0, val)
            mt[h, :, m, :qw] = val
    import ml_dtypes
    return bias.reshape(P, H * NR), mt.reshape(H, P, 4 * 512).astype(ml_dtypes.bfloat16)


class Builder:
    WSLOT_ELEMS = 8192
    NWSLOT = 4
    HBYTES = 48 * 1024

    def __init__(self, cfg):
        self.cfg = c = cfg
        nc = bass.Bass("TRN2", target_bir_lowering=False)
        self.nc = nc
        self.pg = Prog(nc)
        D, F, NT, DC, H = c.D, c.F, c.NT, c.DC, c.H
        self.inputs = {}

        def din(name, shape, dtype=F32):
            ap = nc.dram_tensor(name, list(shape), dtype, kind="ExternalInput").ap()
            self.inputs[name] = (tuple(shape), dtype)
            return ap
        self.x_in = din("x", [NT, D])
        self.y_out = nc.dram_tensor("y", [NT, D], F32, kind="ExternalOutput").ap()
        self.c_col = din("c_col", [P, DC])
        self.ada_w = din("ada_w", [c.depth, D, 9 * D])
        self.ada_b_col = din("ada_b_col", [c.depth, P, 9 * DC])
        self.normg_col = din("normg_col", [c.depth, P, 3 * DC])
        self.w_gate = din("ffn_w_gate", [c.depth, 2, D, F])
        self.w_up = din("ffn_w_up", [c.depth, 2, D, F])
        self.w_down = din("ffn_w_down", [c.depth, 2, F, D])
        self.ident_in = din("ident", [P, P])
        self.bones_in = din("blockones", [P, P])
        na = sum(1 for k in c.kinds if k == 0); ncv = sum(1 for k in c.kinds if k == 1); nh = sum(1 for k in c.kinds if k == 2)
        if na:
            self.attn_w_in = din("attn_w_in", [na, D, 3 * D])
            self.attn_w_out = din("attn_w_out", [na, D, D])
            self.attn_gcol = din("attn_gcol", [na, P, 4])
            self.attn_lam = din("attn_lambda", [na, 1, 4 * 64])
            self.biastab_in = din("biastab", [P, H * c.NR])
            self.mtab_in = din("mtab", [H, P, 4 * 512], BF16)
            self.pfx_flag = din("pfx_flag", [P, 2])
            self.qT_d = nc.dram_tensor("qT_d", [H, P, NT], BF16).ap()
            self.kT_d = nc.dram_tensor("kT_d", [H, P, c.NKEYS], BF16).ap()
            self.V_d = nc.dram_tensor("V_d", [c.NKEYS, D], BF16).ap()
            self.oT_d = nc.dram_tensor("oT_d", [H, P, NT], BF16).ap()
        if ncv:
            self.conv_w_in = din("conv_w_in", [ncv, D, 3 * D])
            self.conv_w_col = din("conv_w_col", [ncv, P, DC * 3])
            self.conv_w_out = din("conv_w_out", [ncv, D, D])
        if nh:
            self.hgrn_w_in = din("hgrn_w_in", [nh, D, 4 * D])
            self.hgrn_w_out = din("hgrn_w_out", [nh, D, D])
            self.hgrn_gcol = din("hgrn_gcol", [nh, P, 1])
            self.hgrn_lb_col = din("hgrn_lb_col", [c.depth, P, DC])
            self.bdtri_in = din("bdtri", [P, P])
            self.scanmask_in = din("scanmask", [P, 512])
        self.xs = nc.dram_tensor("xs_scratch", [NT, D], F32).ap()

        sb = lambda name, shape, dt_: nc.alloc_sbuf_tensor(name, list(shape), dt_).ap()
        T = c.T
        self.wring = sb("wring", [P, self.NWSLOT, self.WSLOT_ELEMS], BF16)
        self.regH = sb("regH", [P, self.HBYTES // 2], BF16)
        self.xnT = sb("xnT", [P, DC, T], BF16)
        self.xin = sb("xin", [P, 2, D], F32)
        self.xsq = sb("xsq", [P, 2, D], BF16)
        self.NXR = min(4, D // 512)
        self.xres = self.xin[:, 1, 0:self.NXR * 512].rearrange("p (s f) -> p s f", f=512)
        if D >= 1024:
            self.ytmp = None
        else:
            self.ytmp = sb("ytmp", [P, 2, 512], F32)
        self.gb = sb("gb", [P, D], F32)
        self.sil = sb("sil", [P, 2, 512], BF16)
        self.identf = sb("identf", [P, P], F32)
        self.identb = sb("identb", [P, P], BF16)
        self.onesf = sb("onesf", [P, P], F32)
        self.onesb = sb("onesb", [P, P], BF16)
        self.bonesb = sb("bonesb", [P, P], BF16)
        self.diag = sb("diag", [P, 2, P], F32)
        self.stat = sb("stat", [P, 16], F32)
        self.epsc = sb("epsc", [P, 2], F32)
        self.ccol = sb("ccol", [P, DC], F32)
        self.condb = sb("condb", [P, DC], BF16)
        self.modcol = sb("modcol", [P, 9 * DC], F32)
        self.modA = sb("modA", [P, 3 * DC], F32)
        self.adab = sb("adab", [P, 9 * DC], F32)
        self.ngc = sb("ngc", [P, 3 * DC], F32)
        self.small = sb("small", [P, 64], F32)
        self.lamrow = sb("lamrow", [1, 4 * 64 + 16], F32)
        if na:
            self.biastab = sb("biastab_s", [P, H * c.NR], F32)
            if c.NPFX:
                self.biastabP = sb("biastabP_s", [P, H * c.NR], F32)
            else:
                self.biastabP = self.biastab
        if ncv:
            self.cwcol = sb("cwcol", [P, DC * 3], F32)
            self.halo = sb("halo", [P, DC, 2], F32)
        if nh:
            self.Sst = sb("Sst", [P, H, P], F32)
            self.Sbf = sb("Sbf", [P, H, P], BF16)
            self.bdtri = sb("bdtri_s", [P, P], BF16)
            self.scanmask = sb("scanmask_s", [P, 512], F32)
            self.lbt = sb("lbt", [P, c.depth + 6, DC], F32)
            self.eblt = sb("eblt", [P, 4, 8], F32)
        self.psum = nc.alloc_psum_tensor("psum", [P, 8, 512], F32).ap()
        self.wslot_n = 0
        self._hviews = {}

    def hv(self, name, off, shape, dtype):
        esz = 4 if dtype == F32 else 2
        n = 1
        for s_ in shape[1:]:
            n *= s_
        assert off % 4 == 0 and off + n * esz <= self.HBYTES, (name, off, n * esz)
        ap = self.regH[:, off // 2: off // 2 + n * esz // 2]
        if dtype == F32:
            ap = ap.bitcast(F32)
        if len(shape) == 3:
            ap = ap.rearrange("p (a b) -> p a b", b=shape[2])
        self.pg.region_of[name] = "H"
        return ap

    def next_wslot(self):
        s = self.wslot_n % self.NWSLOT
        self.wslot_n += 1
        return s

    def wkeys(self, s):
        return tuple(("w", s, i) for i in range(4))

    def load_w(self, w2d, c0, ncn, f0, nf, slot=None, part=None, eoff=0):
        if slot is None:
            slot = self.next_wslot()
        assert eoff + ncn * nf <= self.WSLOT_ELEMS
        view = self.wring[:, slot, eoff:eoff + ncn * nf].rearrange("p (c f) -> p c f", f=nf)
        src = w2d.rearrange("(c p) f -> p c f", p=P)[:, c0:c0 + ncn, f0:f0 + nf]
        keys = self.wkeys(slot) if part is None else (("w", slot, part),)
        self.pg.add("pool", lambda g, view=view, src=src: g.dma_start(out=view, in_=src), writes=keys, dma=True)
        return view, keys

    def xinkeys(self, sl):
        return tuple(("xin", sl, i) for i in range(4))

    def xkeys(self, gt):
        return tuple(("x", gt, d_) for d_ in range(self.cfg.D // 512))

    def consts(self):
        pg = self.pg
        pg.add("sp", lambda q: q.dma_start(out=self.identf, in_=self.ident_in), writes=("identf",), dma=True)
        pg.add("sp", lambda q: q.dma_start(out=self.onesf, in_=self.bones_in), writes=("onesf",), dma=True)
        pg.add("sp", lambda q: q.dma_start(out=self.ccol, in_=self.c_col), writes=("ccol",), dma=True)
        pg.add("dve", lambda v: v.tensor_copy(out=self.identb, in_=self.identf), reads=("identf",), writes=("identb",))
        pg.add("dve", lambda v: v.tensor_copy(out=self.bonesb, in_=self.onesf), reads=("onesf",), writes=("bonesb",))
        pg.add("dve", lambda v: v.memset(self.onesf, 1.0), reads=("bonesb",), writes=("onesf",))
        pg.add("dve", lambda v: v.memset(self.onesb, 1.0), writes=("onesb",))
        pg.add("dve", lambda v: v.memset(self.epsc, RMS_EPS), writes=("epsc",))
        pg.add("act", lambda a: a.activation(out=self.condb, in_=self.ccol, func=AF.Silu), reads=("ccol",), writes=("condb",))
        if hasattr(self, "biastab"):
            pg.add("sp", lambda q: q.dma_start(out=self.biastab, in_=self.biastab_in), writes=("biastab",), dma=True)
            pg.add("sp", lambda q: q.dma_start(out=self.small[:, 60:62], in_=self.pfx_flag), writes=("pfxflag",), dma=True)
            if self.cfg.NPFX:
                pg.add("dve", lambda v: v.tensor_scalar(out=self.biastabP, in0=self.biastab, scalar1=self.small[:, 60:61], scalar2=None, op0=ALU.add),
                       reads=("biastab", "pfxflag"), writes=("biastabP",))

    def copy_x_in(self):
        NT = self.cfg.NT
        step = 512
        for t0 in range(0, NT, step):
            n = min(step, NT - t0)
            keys = tuple(k for i in range((n + P - 1) // P) for k in self.xkeys(t0 // P + i))
            self.pg.add("sp", lambda q, t0=t0, n=n: q.dma_start(out=self.xs[t0:t0 + n, :], in_=self.x_in[t0:t0 + n, :]),
                        writes=keys, dma=True)

    def finish(self):
        NT = self.cfg.NT
        step = 512
        for t0 in range(0, NT, step):
            n = min(step, NT - t0)
            keys = tuple(k for i in range((n + P - 1) // P) for k in self.xkeys(t0 // P + i))
            self.pg.add("sp", lambda q, t0=t0, n=n: q.dma_start(out=self.y_out[t0:t0 + n, :], in_=self.xs[t0:t0 + n, :]),
                        reads=keys, writes=(("yout", t0),), dma=True)

    def modulation(self, layer):
        c = self.cfg; pg = self.pg; DC = c.DC
        pg.add("sp", lambda q: q.dma_start(out=self.adab, in_=self.ada_b_col[layer]), writes=("adab",), dma=True)
        pg.add("sp", lambda q: q.dma_start(out=self.ngc, in_=self.normg_col[layer]), writes=("ngc",), dma=True)
        W = self.ada_w[layer]
        ncols = 9 * c.D
        grp = min(self.WSLOT_ELEMS // DC, 512)
        bankid = 7
        pkey = ("ps", bankid)
        for g0 in range(0, ncols, grp):
            ng = min(grp, ncols - g0)
            view, wkey = self.load_w(W, 0, DC, g0, ng)
            for jj in range(ng // P):
                j = g0 // P + jj
                for dc in range(DC):
                    pg.add("pe", lambda t, view=view, jj=jj, dc=dc, j=j: t.matmul(self.psum[:, bankid, j:j + 1], view[:, dc, jj * P:(jj + 1) * P], self.condb[:, dc:dc + 1],
                                                                                 start=(dc == 0), stop=(dc == DC - 1)),
                           reads=wkey + ("condb",), writes=(pkey,))
        pg.add("dve", lambda v: v.tensor_tensor(out=self.modcol, in0=self.psum[:, bankid, 0:9 * DC], in1=self.adab, op=ALU.add),
               reads=("adab",), writes=(pkey, "modcol"))
        for s in range(3):
            pg.add("dve", lambda v, s=s: v.scalar_tensor_tensor(out=self.modA[:, s * DC:(s + 1) * DC], in0=self.modcol[:, (3 * s + 1) * DC:(3 * s + 2) * DC],
                                                                scalar=1.0, in1=self.ngc[:, s * DC:(s + 1) * DC], op0=ALU.add, op1=ALU.mult),
                   reads=("modcol", "ngc"), writes=(("modA", s),))

    def gate_bcast(self, s, factor):
        c = self.cfg; pg = self.pg; DC = c.DC
        gcol0 = (3 * s + 2) * DC
        for j4 in range(0, DC, 4):
            bankid = 6
            pkey = ("ps", bankid)
            nj = min(4, DC - j4)
            for jj in range(nj):
                j = j4 + jj
                dslot = j % 2
                pg.add("dve", lambda v, j=j, dslot=dslot: v.tensor_scalar(out=self.diag[:, dslot, :], in0=self.identf, scalar1=self.modcol[:, gcol0 + j:gcol0 + j + 1],
                                                                         scalar2=float(factor), op0=ALU.mult, op1=ALU.mult),
                       reads=("identf", "modcol"), writes=(("diag", dslot),))
                pg.add("pe", lambda t, jj=jj, dslot=dslot: t.matmul(self.psum[:, bankid, jj * P:(jj + 1) * P], self.onesf, self.diag[:, dslot, :], start=True, stop=True),
                       reads=("onesf", ("diag", dslot)), writes=(pkey,))
            pg.add("act", lambda a, j4=j4, nj=nj: a.copy(out=self.gb[:, j4 * P:(j4 + nj) * P], in_=self.psum[:, bankid, 0:nj * P]),
                   writes=(pkey, ("gb", j4 // 4)))

    def norm_phase(self, s, t0, T=None):
        c = self.cfg; pg = self.pg; DC = c.DC; D = c.D
        T = T or c.T
        shcol0 = (3 * s) * DC
        ntile = T // P
        for q0 in range(0, ntile, 2):
            nq = min(2, ntile - q0)
            for qi in range(nq):
                ti = q0 + qi
                gt = t0 // P + ti
                sl = ti % 2
                pg.add("sp", lambda q, gt=gt, sl=sl: q.dma_start(out=self.xin[:, sl, :], in_=self.xs[gt * P:(gt + 1) * P, :]),
                       reads=self.xkeys(gt), writes=self.xinkeys(sl), dma=True)
                st = ti % 4
                pg.add("act", lambda a, sl=sl, st=st, qi=qi: a.activation(out=self.xsq[:, qi, :], in_=self.xin[:, sl, :], func=AF.Square, accum_out=self.stat[:, st:st + 1]),
                       reads=self.xinkeys(sl), writes=(("xsq", qi), ("stat", st)))
                pg.add("act", lambda a, st=st: a.activation(out=self.stat[:, 4 + st:5 + st], in_=self.stat[:, st:st + 1], func=AF.Sqrt, scale=1.0 / D, bias=self.epsc[:, 0:1]),
                       reads=(("stat", st), "epsc"), writes=(("stat", 4 + st),))
                pg.add("dve", lambda v, st=st: v.reciprocal(out=self.stat[:, 8 + st:9 + st], in_=self.stat[:, 4 + st:5 + st]),
                       reads=(("stat", 4 + st),), writes=(("stat", 8 + st),))
                pg.add("dve", lambda v, sl=sl, st=st, qi=qi: v.tensor_scalar(out=self.xsq[:, qi, :], in0=self.xin[:, sl, :], scalar1=self.stat[:, 8 + st:9 + st], scalar2=None, op0=ALU.mult),
                       reads=self.xinkeys(sl) + (("stat", 8 + st),), writes=(("xsq", qi),))
            for dc in range(DC):
                bankid = 4 + (dc % 2)
                pkey = ("ps", bankid)
                pbf = self.psum[:, bankid, :].bitcast(BF16)
                for qi in range(nq):
                    pg.add("pe", lambda t, qi=qi, dc=dc, pbf=pbf: t.transpose(pbf[:, qi * P:(qi + 1) * P], self.xsq[:, qi, dc * P:(dc + 1) * P], self.identb),
                           reads=(("xsq", qi), "identb"), writes=(pkey,))
                tok0 = q0 * P
                outv = self.xnT[:, dc, tok0:tok0 + nq * P]
                wk = (("xnT", dc, q0 // 4, (q0 // 2) % 2),)
                if dc % 2 == 0:
                    pg.add("act", lambda a, outv=outv, pbf=pbf, nq=nq, dc=dc: a.activation(out=outv, in_=pbf[:, 0:nq * P], func=AF.Identity,
                                                                                      bias=self.modcol[:, shcol0 + dc:shcol0 + dc + 1], scale=self.modA[:, s * DC + dc:s * DC + dc + 1]),
                           reads=("modcol", ("modA", s)), writes=(pkey,) + wk)
                else:
                    pg.add("dve", lambda v, outv=outv, pbf=pbf, nq=nq, dc=dc: v.tensor_scalar(out=outv, in0=pbf[:, 0:nq * P], scalar1=self.modA[:, s * DC + dc:s * DC + dc + 1],
                                                                                         scalar2=self.modcol[:, shcol0 + dc:shcol0 + dc + 1], op0=ALU.mult, op1=ALU.add),
                           reads=("modcol", ("modA", s)), writes=(pkey,) + wk)

    def resid_phase(self, t0, kchunks, lhs_fn, lhs_keys_fn, w2d, wc0, T=None):
        c = self.cfg; pg = self.pg; D = c.D
        T = T or c.T
        ntile = T // P
        ndt = D // 512
        KG = min(self.WSLOT_ELEMS // 512, kchunks)
        if kchunks > 16:
            KG = (kchunks + 1) // 2
        blocks = [(dt_, tt) for dt_ in range(ndt) for tt in range(ntile)]
        NXR = self.NXR
        LOOK = min(2, NXR - 1)

        def ytv(ys):
            if self.ytmp is None:
                return self.xsq[:, ys, 0:1024].bitcast(F32)
            return self.ytmp[:, ys, :]
        ykey = (lambda ys: ("xsq", ys)) if self.ytmp is None else (lambda ys: ("ytmp", ys))

        def issue_load(bi):
            dt_, tt = blocks[bi]
            gt = t0 // P + tt
            sl = bi % NXR
            pg.add("sp", lambda q, gt=gt, dt_=dt_, sl=sl: q.dma_start(out=self.xres[:, sl, :], in_=self.xs[gt * P:(gt + 1) * P, dt_ * 512:(dt_ + 1) * 512]),
                   reads=(("x", gt, dt_),), writes=(("xin", 1, sl),), dma=True)
        for bi in range(min(LOOK, len(blocks))):
            issue_load(bi)
        wviews = None
        for bi, (dt_, tt) in enumerate(blocks):
            if LOOK == 0:
                issue_load(bi)
            if tt == 0:
                wviews = []
                for k0 in range(0, kchunks, KG):
                    nk = min(KG, kchunks - k0)
                    wviews.append((k0, nk) + self.load_w(w2d, wc0 + k0, nk, dt_ * 512, 512))
            bankid = bi % 4
            pkey = ("ps", bankid)
            for (k0, nk, view, wkey) in wviews:
                for kk in range(nk):
                    kc = k0 + kk
                    pg.add("pe", lambda t, view=view, kk=kk, kc=kc, tt=tt, bankid=bankid: t.matmul(self.psum[:, bankid, :], lhs_fn(kc, tt), view[:, kk, :],
                                                                                            start=(kc == 0), stop=(kc == kchunks - 1)),
                           reads=wkey + tuple(lhs_keys_fn(kc, tt)), writes=(pkey,))
            gt = t0 // P + tt
            sl = bi % NXR
            ys = bi % 2
            pg.add("dve", lambda v, bankid=bankid, dt_=dt_, ys=ys: v.tensor_tensor(out=ytv(ys), in0=self.psum[:, bankid, :], in1=self.gb[:, dt_ * 512:(dt_ + 1) * 512], op=ALU.mult),
                   reads=(("gb", dt_),), writes=(pkey, ykey(ys)))
            pg.add("pool", lambda g, sl=sl, ys=ys: g.tensor_tensor(out=self.xres[:, sl, :], in0=self.xres[:, sl, :], in1=ytv(ys), op=ALU.add),
                   reads=(ykey(ys), ("xin", 1, sl)), writes=(("xin", 1, sl),))
            if LOOK > 0 and bi + LOOK < len(blocks):
                issue_load(bi + LOOK)
            pg.add("sp", lambda q, gt=gt, dt_=dt_, sl=sl: q.dma_start(out=self.xs[gt * P:(gt + 1) * P, dt_ * 512:(dt_ + 1) * 512], in_=self.xres[:, sl, :]),
                   reads=(("xin", 1, sl),), writes=(("x", gt, dt_),), dma=True)

    def ffn(self, layer, which, s, t0):
        c = self.cfg; pg = self.pg; DC = c.DC; T = c.T
        Wg = self.w_gate[layer, which]; Wu = self.w_up[layer, which]; Wd = self.w_down[layer, which]
        hT = self.hv("hT", 0, [P, 24, T], BF16)
        self.norm_phase(s, t0)
        nhalf = (T + 511) // 512
        for (fc0, nfc) in c.fparts:
            for g0 in range(0, nfc, 4):
                ng = min(4, nfc - g0)
                gv, gk = self.load_w(Wg, 0, DC, (fc0 + g0) * P, ng * P)
                uv, uk = self.load_w(Wu, 0, DC, (fc0 + g0) * P, ng * P)
                for jj in range(ng):
                    fl = g0 + jj
                    for hh in range(nhalf):
                        tk0 = hh * 512; ntk = min(512, T - tk0)
                        par = (fl * nhalf + hh) % 2
                        gb_, ub_ = (0, 1) if par == 0 else (2, 3)
                        for dc in range(DC):
                            pg.add("pe", lambda t, gv=gv, jj=jj, dc=dc, tk0=tk0, ntk=ntk, gb_=gb_: t.matmul(self.psum[:, gb_, 0:ntk], gv[:, dc, jj * P:(jj + 1) * P], self.xnT[:, dc, tk0:tk0 + ntk],
                                                                                                   start=(dc == 0), stop=(dc == DC - 1)),
                                   reads=gk + (("xnT", dc, hh, 0), ("xnT", dc, hh, 1)), writes=(("ps", gb_),))
                        for dc in range(DC):
                            pg.add("pe", lambda t, uv=uv, jj=jj, dc=dc, tk0=tk0, ntk=ntk, ub_=ub_: t.matmul(self.psum[:, ub_, 0:ntk], uv[:, dc, jj * P:(jj + 1) * P], self.xnT[:, dc, tk0:tk0 + ntk],
                                                                                                   start=(dc == 0), stop=(dc == DC - 1)),
                                   reads=uk + (("xnT", dc, hh, 0), ("xnT", dc, hh, 1)), writes=(("ps", ub_),))
                        pg.add("act", lambda a, gb_=gb_, ntk=ntk, par=par: a.activation(out=self.sil[:, par, 0:ntk], in_=self.psum[:, gb_, 0:ntk], func=AF.Silu),
                               writes=(("ps", gb_), ("sil", par)))
                        pg.add("dve", lambda v, ub_=ub_, ntk=ntk, par=par, fl=fl, tk0=tk0: v.tensor_tensor(out=hT[:, fl, tk0:tk0 + ntk], in0=self.psum[:, ub_, 0:ntk], in1=self.sil[:, par, 0:ntk], op=ALU.mult),
                               reads=(("sil", par),), writes=(("ps", ub_), ("hT", fl, hh)))
            self.resid_phase(t0, nfc, lambda kc, tt: hT[:, kc, tt * P:(tt + 1) * P], lambda kc, tt: (("hT", kc, tt // 4),), Wd, fc0)
        pg.end_phase("H")

    def conv_prep(self, slot):
        pg = self.pg
        pg.add("sp", lambda q: q.dma_start(out=self.cwcol, in_=self.conv_w_col[slot]), writes=("cwcol",), dma=True)
        pg.add("dve", lambda v: v.memset(self.halo, 0.0), writes=("halo",))

    def conv_mixer(self, slot, t0):
        c = self.cfg; pg = self.pg; DC = c.DC; T = c.T; D = c.D
        Win = self.conv_w_in[slot]; Wout = self.conv_w_out[slot]
        zT = self.hv("zT", 0, [P, DC, T], BF16)
        vbuf = self.hv("vbuf", DC * T * 2, [P, 2, T + 2], F32)
        o2 = DC * T * 2 + 2 * (T + 2) * 4
        csb = self.hv("csb", o2, [P, 2, 512], F32)
        ybuf = self.hv("ybuf", o2 + 4096, [P, 1, 512], F32)
        self.norm_phase(1, t0)
        nhalf = (T + 511) // 512
        banksets = ((0, 1, 2), (3, 6, 7))
        it = 0
        for j in range(DC):
            slot_w = self.next_wslot()
            views = []
            for part in range(3):
                vw, kk = self.load_w(Win, 0, DC, part * D + j * P, P, slot=slot_w, part=part, eoff=part * DC * P)
                views.append((vw, kk))
            vb = j % 2
            pg.add("pool", lambda g, vb=vb, j=j: g.tensor_copy(out=vbuf[:, vb, 0:2], in_=self.halo[:, j, :]), reads=("halo",), writes=(("vbuf", vb, -1),))
            for hh in range(nhalf):
                tk0 = hh * 512; ntk = min(512, T - tk0)
                bs = banksets[it % 2]; it += 1
                for part in range(3):
                    vw, kk = views[part]
                    for dc in range(DC):
                        pg.add("pe", lambda t, vw=vw, dc=dc, tk0=tk0, ntk=ntk, b_=bs[part]: t.matmul(self.psum[:, b_, 0:ntk], vw[:, dc, :], self.xnT[:, dc, tk0:tk0 + ntk],
                                                                                                start=(dc == 0), stop=(dc == DC - 1)),
                               reads=kk + (("xnT", dc, hh, 0), ("xnT", dc, hh, 1)), writes=(("ps", bs[part]),))
                cs = hh % 2
                pg.add("act", lambda a, cs=cs, ntk=ntk, b_=bs[1]: a.copy(out=csb[:, cs, 0:ntk], in_=self.psum[:, b_, 0:ntk]), writes=(("ps", bs[1]), ("csb", cs)))
                pg.add("dve", lambda v, cs=cs, ntk=ntk, tk0=tk0, vb=vb, b_=bs[2]: v.tensor_tensor(out=vbuf[:, vb, 2 + tk0:2 + tk0 + ntk], in0=self.psum[:, b_, 0:ntk], in1=csb[:, cs, 0:ntk], op=ALU.mult),
                       reads=(("csb", cs),), writes=(("ps", bs[2]), ("vbuf", vb, hh)))
                rk = (("vbuf", vb, hh), ("vbuf", vb, hh - 1))
                pg.add("pool", lambda g, cs=cs, ntk=ntk, tk0=tk0, vb=vb, j=j: g.tensor_scalar(out=ybuf[:, 0, 0:ntk], in0=vbuf[:, vb, 2 + tk0:2 + tk0 + ntk], scalar1=self.cwcol[:, j * 3 + 2:j * 3 + 3], scalar2=None, op0=ALU.mult),
                       reads=rk + ("cwcol",), writes=(("ybuf", 0),))
                pg.add("dve", lambda g, cs=cs, ntk=ntk, tk0=tk0, vb=vb, j=j: g.scalar_tensor_tensor(out=ybuf[:, 0, 0:ntk], in0=vbuf[:, vb, 1 + tk0:1 + tk0 + ntk], scalar=self.cwcol[:, j * 3 + 1:j * 3 + 2], in1=ybuf[:, 0, 0:ntk], op0=ALU.mult, op1=ALU.add),
                       reads=rk + ("cwcol", ("ybuf", 0)), writes=(("ybuf", 0),))
                pg.add("dve", lambda g, cs=cs, ntk=ntk, tk0=tk0, vb=vb, j=j: g.scalar_tensor_tensor(out=ybuf[:, 0, 0:ntk], in0=vbuf[:, vb, tk0:tk0 + ntk], scalar=self.cwcol[:, j * 3:j * 3 + 1], in1=ybuf[:, 0, 0:ntk], op0=ALU.mult, op1=ALU.add),
                       reads=rk + ("cwcol", ("ybuf", 0)), writes=(("ybuf", 0),))
                pg.add("dve", lambda v, cs=cs, ntk=ntk, tk0=tk0, j=j, b_=bs[0]: v.tensor_tensor(out=zT[:, j, tk0:tk0 + ntk], in0=self.psum[:, b_, 0:ntk], in1=ybuf[:, 0, 0:ntk], op=ALU.mult),
                       reads=(("ybuf", 0),), writes=(("ps", bs[0]), ("zT", j, hh)))
            pg.add("pool", lambda g, vb=vb, j=j: g.tensor_copy(out=self.halo[:, j, :], in_=vbuf[:, vb, T:T + 2]), reads=(("vbuf", vb, nhalf - 1),), writes=("halo",))
        self.resid_phase(t0, DC, lambda kc, tt: zT[:, kc, tt * P:(tt + 1) * P], lambda kc, tt: (("zT", kc, tt // 4),), Wout, 0)
        pg.end_phase("H")

    def attn_prep(self, slot, layer):
        c = self.cfg; pg = self.pg
        sm = self.small
        pg.add("sp", lambda q: q.dma_start(out=sm[:, 0:4], in_=self.attn_gcol[slot]), writes=("agcol",), dma=True)
        pg.add("sp", lambda q: q.dma_start(out=self.lamrow[:, 0:256], in_=self.attn_lam[slot]), writes=("lamrow",), dma=True)
        lr = self.lamrow
        pg.add("dve", lambda v: v.tensor_scalar(out=sm[:, 4:5], in0=sm[:, 0:1], scalar1=0.125, scalar2=None, op0=ALU.mult), reads=("agcol",), writes=("qgs",))
        pg.add("dve", lambda v: v.tensor_scalar(out=sm[:, 5:6], in0=sm[:, 2:3], scalar1=float(1.0 - c.lambda_init(layer)), scalar2=None, op0=ALU.mult), reads=("agcol",), writes=("sgs",))
        pg.add("dve", lambda v: v.tensor_tensor(out=lr[:, 0:64], in0=lr[:, 0:64], in1=lr[:, 64:128], op=ALU.mult), reads=("lamrow",), writes=("lamrow",))
        pg.add("dve", lambda v: v.tensor_tensor(out=lr[:, 128:192], in0=lr[:, 128:192], in1=lr[:, 192:256], op=ALU.mult), reads=("lamrow",), writes=("lamrow",))
        pg.add("dve", lambda v: v.reduce_sum(out=lr[:, 256:257], in_=lr[:, 0:64], axis=mybir.AxisListType.X), reads=("lamrow",), writes=("lamrow",))
        pg.add("dve", lambda v: v.reduce_sum(out=lr[:, 257:258], in_=lr[:, 128:192], axis=mybir.AxisListType.X), reads=("lamrow",), writes=("lamrow",))
        pg.add("act", lambda a: a.activation(out=lr[:, 258:260], in_=lr[:, 256:258], func=AF.Exp), reads=("lamrow",), writes=("lamrow",))
        pg.add("dve", lambda v: v.scalar_tensor_tensor(out=lr[:, 260:261], in0=lr[:, 259:260], scalar=float(-c.lambda_init(layer)), in1=lr[:, 258:259], op0=ALU.add, op1=ALU.subtract),
               reads=("lamrow",), writes=("lamrow",))
        pkey = ("ps", 7)
        pg.add("pe", lambda t: t.matmul(self.psum[:, 7, 0:1], self.onesf[0:1, :], lr[0:1, 260:261], start=True, stop=True), reads=("onesf", "lamrow"), writes=(pkey,))
        pg.add("dve", lambda v: v.tensor_copy(out=sm[:, 6:7], in_=self.psum[:, 7, 0:1]), writes=(pkey, "neglam"))

    def attn_qkv(self, slot, t0):
        c = self.cfg; pg = self.pg; DC = c.DC; T = c.T; D = c.D; H = c.H
        Win = self.attn_w_in[slot]
        sm = self.small
        sq = self.hv("sq", 0, [P, 2, 512], BF16)
        rs = self.hv("rs", 2048, [P, 2, 512], F32)
        rr = self.hv("rr", 2048 + 4096, [P, 2, 512], F32)
        stg = self.hv("stg", 2048 + 8192, [P, 4, 512], BF16)
        vst = self.hv("vst", 2048 + 8192 + 4096, [P, 2, 512], BF16)
        self.norm_phase(1, t0)
        nhalf = (T + 511) // 512
        it = 0
        for h4 in range(0, H, 4):
            nh = min(4, H - h4)
            qv, qk = self.load_w(Win, 0, DC, h4 * P, nh * P)
            kv, kk = self.load_w(Win, 0, DC, D + h4 * P, nh * P)
            for jj in range(nh):
                h = h4 + jj
                for hh in range(nhalf):
                    tk0 = hh * 512; ntk = min(512, T - tk0)
                    for which, (wv, wk_) in enumerate(((qv, qk), (kv, kk))):
                        pb = (0, 1)[it % 2]; sb_ = (2, 3)[it % 2]; par = it % 2; it += 1
                        for dc in range(DC):
                            pg.add("pe", lambda t, wv=wv, jj=jj, dc=dc, tk0=tk0, ntk=ntk, pb=pb: t.matmul(self.psum[:, pb, 0:ntk], wv[:, dc, jj * P:(jj + 1) * P], self.xnT[:, dc, tk0:tk0 + ntk],
                                                                                                 start=(dc == 0), stop=(dc == DC - 1)),
                                   reads=wk_ + (("xnT", dc, hh, 0), ("xnT", dc, hh, 1)), writes=(("ps", pb),))
                        pg.add("act", lambda a, pb=pb, ntk=ntk, par=par: a.activation(out=sq[:, par, 0:ntk], in_=self.psum[:, pb, 0:ntk], func=AF.Square),
                               writes=(("ps", pb), ("sq", par)))
                        pg.add("pe", lambda t, sb_=sb_, ntk=ntk, par=par: t.matmul(self.psum[:, sb_, 0:ntk], self.bonesb, sq[:, par, 0:ntk], start=True, stop=True),
                               reads=("bonesb", ("sq", par)), writes=(("ps", sb_),))
                        pg.add("act", lambda a, sb_=sb_, ntk=ntk, par=par: a.activation(out=rs[:, par, 0:ntk], in_=self.psum[:, sb_, 0:ntk], func=AF.Sqrt, scale=1.0 / 64, bias=self.epsc[:, 0:1]),
                               reads=("epsc",), writes=(("ps", sb_), ("rs", par)))
                        pg.add("dve", lambda v, ntk=ntk, par=par: v.reciprocal(out=rr[:, par, 0:ntk], in_=rs[:, par, 0:ntk]), reads=(("rs", par),), writes=(("rr", par),))
                        gcol = sm[:, 4:5] if which == 0 else sm[:, 1:2]
                        ss = (it - 1) % 4
                        pg.add("dve", lambda v, pb=pb, ntk=ntk, par=par, gcol=gcol, ss=ss: v.scalar_tensor_tensor(out=stg[:, ss, 0:ntk], in0=self.psum[:, pb, 0:ntk], scalar=gcol, in1=rr[:, par, 0:ntk], op0=ALU.mult, op1=ALU.mult),
                               reads=(("rr", par), "qgs", "agcol"), writes=(("ps", pb), ("stg", ss)))
                        if which == 0:
                            dst = self.qT_d[h, :, t0 + tk0:t0 + tk0 + ntk]; dk = ("qT_d", h, (t0 + tk0) // 512)
                        else:
                            dst = self.kT_d[h, :, c.NPFX + t0 + tk0:c.NPFX + t0 + tk0 + ntk]; dk = ("kT_d", h, (c.NPFX + t0 + tk0) // 512)
                        pg.add("sp", lambda q, dst=dst, ss=ss, ntk=ntk: q.dma_start(out=dst, in_=stg[:, ss, 0:ntk]), reads=(("stg", ss),), writes=(dk,), dma=True)
        ntile = T // P
        it = 0
        for cg in range(D // 512):
            vv, vk = self.load_w(Win, 0, DC, 2 * D + cg * 512, 512)
            for tt in range(ntile):
                pb = 4 + it % 2; par = it % 2; it += 1
                for dc in range(DC):
                    pg.add("pe", lambda t, vv=vv, dc=dc, tt=tt, pb=pb: t.matmul(self.psum[:, pb, :], self.xnT[:, dc, tt * P:(tt + 1) * P], vv[:, dc, :], start=(dc == 0), stop=(dc == DC - 1)),
                           reads=vk + (("xnT", dc, tt // 4, (tt // 2) % 2),), writes=(("ps", pb),))
                if par == 0:
                    pg.add("act", lambda a, pb=pb, par=par: a.copy(out=vst[:, par, :], in_=self.psum[:, pb, :]), writes=(("ps", pb), ("vst", par)))
                else:
                    pg.add("dve", lambda v, pb=pb, par=par: v.tensor_copy(out=vst[:, par, :], in_=self.psum[:, pb, :]), writes=(("ps", pb), ("vst", par)))
                gk = (c.NPFX + t0) // P + tt
                pg.add("sp", lambda q, gk=gk, cg=cg, par=par: q.dma_start(out=self.V_d[gk * P:(gk + 1) * P, cg * 512:(cg + 1) * 512], in_=vst[:, par, :]),
                       reads=(("vst", par),), writes=(("V_d", gk, cg),), dma=True)
        pg.end_phase("H")

    def attn_main(self, slot, layer):
        c = self.cfg; pg = self.pg; H = c.H; NT = c.NT; NK = c.NKEYS; NPFX = c.NPFX
        sm = self.small
        NKT = NK // P
        qts = self.hv("qts", 0, [P, 4, 512], BF16)
        o = 4096
        qctr = [0]
        E = self.hv("E", o, [P, 4, 512], BF16); o += 4096
        mt = self.hv("mt", o, [P, 2, 2048], BF16); o += 8192
        rcp = self.hv("rcp", o, [P, 2, 512], F32); o += 4096
        tt_ = self.hv("tt", o, [P, 2, 512], F32); o += 4096
        av = self.hv("av", o, [P, 2, 512], F32); o += 4096
        a2 = self.hv("a2", o, [P, 2, 512], BF16); o += 2048
        rs2 = self.hv("rs2", o, [P, 2, 512], F32); o += 4096
        ob = self.hv("ob", o, [P, 2, 512], BF16); o += 2048
        assert o <= self.HBYTES, o
        pending = []
        epi_n = [0]

        def flush_pending(bk):
            while pending:
                pending.pop(0)(bk)

        def do_head(h):
            qw = c.QW[h]; sl = c.slopes[h]; win = c.WIN[h]
            hb = h % 2
            ks = self.next_wslot(); vs = self.next_wslot()
            kview = self.wring[:, ks, 0:NK]
            vview = self.wring[:, vs, 0:NKT * P].rearrange("p (k d) -> p k d", d=P)
            kkeys = self.wkeys(ks); vkeys = self.wkeys(vs)
            kT_keys = tuple(("kT_d", h, i) for i in range((NK + 511) // 512))
            pg.add("pool", lambda g, kview=kview, h=h: g.dma_start(out=kview, in_=self.kT_d[h, :, :]), reads=kT_keys, writes=kkeys, dma=True)
            V_keys = tuple(("V_d", gk, h // 4) for gk in range(NKT))
            vsrc = self.V_d.rearrange("(kt p) d -> p kt d", p=P)[:, :, h * P:(h + 1) * P]
            pg.add("pool", lambda g, vview=vview, vsrc=vsrc: g.dma_start(out=vview, in_=vsrc), reads=V_keys, writes=vkeys, dma=True)
            qslot = {}

            def load_q(qt_):
                if qt_ >= NT // qw or qt_ in qslot:
                    return
                sl_ = qctr[0] % 4; qctr[0] += 1
                qslot[qt_] = sl_
                Q0_ = qt_ * qw
                pg.add("sp", lambda q, sl_=sl_, Q0_=Q0_: q.dma_start(out=qts[:, sl_, 0:qw], in_=self.qT_d[h, :, Q0_:Q0_ + qw]),
                       reads=tuple(("qT_d", h, i) for i in range(Q0_ // 512, (Q0_ + qw + 511) // 512)), writes=(("qts", sl_),), dma=True)
            load_q(0); load_q(1)
            pg.add("sp", lambda q, h=h, hb=hb: q.dma_start(out=mt[:, hb, :], in_=self.mtab_in[h]), writes=(("mt", hb),), dma=True)
            mtv = mt[:, hb, :].rearrange("p (m q) -> p m q", q=512)
            work = []
            for qt in range(NT // qw):
                Q0 = qt * qw
                gq0 = NPFX + Q0
                kt_max = (gq0 + qw) // P - 1
                kt_min = max(0, (gq0 - win) // P)
                kts = list(range(kt_min, kt_max + 1))
                for i, kt in enumerate(kts):
                    work.append((qt, Q0, kt, i == 0, i == len(kts) - 1))
            nwork = len(work)

            def emit_st(wi):
                qt, Q0, kt, first, last = work[wi]
                K0 = kt * P
                gq0 = NPFX + Q0
                m = (K0 - gq0) // P
                qc0 = max(0, m) * P
                par = wi % 2
                if first:
                    load_q(qt + 2)
                sl_ = qslot[qt]
                for j in range(2):
                    b_ = 2 * par + j
                    pg.add("pe", lambda t, j=j, b_=b_, K0=K0, sl_=sl_, qc0=qc0, kview=kview: t.matmul(self.psum[:, b_, qc0:qw], kview[64 * j:64 * j + 64, K0:K0 + P], qts[64 * j:64 * j + 64, sl_, qc0:qw],
                                                                                                 start=True, stop=True),
                           reads=kkeys + (("qts", sl_),), writes=(("ps", b_),))

            def emit_rest(wi):
                qt, Q0, kt, first, last = work[wi]
                K0 = kt * P
                gq0 = NPFX + Q0
                m = (K0 - gq0) // P
                qc0 = max(0, m) * P
                par = wi % 2
                r = m + c.NREL_NEG
                btab = self.biastabP if K0 < NPFX else self.biastab
                bkey = "biastabP" if K0 < NPFX else "biastab"
                for j in range(2):
                    b_ = 2 * par + j
                    e_ = 2 * par + j
                    pg.add("act", lambda a, b_=b_, e_=e_, qc0=qc0, r=r, btab=btab: a.activation(out=E[:, e_, qc0:qw], in_=self.psum[:, b_, qc0:qw], func=AF.Exp, bias=btab[:, h * c.NR + r:h * c.NR + r + 1], scale=1.0),
                           reads=(bkey,), writes=(("ps", b_), ("E", e_)))
                    if m >= 0:
                        eng = "pool" if j == 0 else "dve"
                        pg.add(eng, lambda g, e_=e_, qc0=qc0, m=m: g.tensor_tensor(out=E[:, e_, qc0:qw], in0=E[:, e_, qc0:qw], in1=mtv[:, m, qc0:qw], op=ALU.mult),
                               reads=(("E", e_), ("mt", hb)), writes=(("E", e_),))
                for j in range(2):
                    e_ = 2 * par + j
                    pg.add("pe", lambda t, j=j, e_=e_, kt=kt, qc0=qc0, first=first, last=last, vview=vview: t.matmul(self.psum[:, 4 + j, qc0:qw], vview[:, kt, :], E[:, e_, qc0:qw], start=first, stop=last),
                           reads=vkeys + (("E", e_),), writes=(("ps", 4 + j),))
                    pg.add("pe", lambda t, j=j, e_=e_, qc0=qc0, first=first, last=last: t.matmul(self.psum[:, 6 + j, qc0:qw], self.onesb, E[:, e_, qc0:qw], start=first, stop=last),
                           reads=("onesb", ("E", e_)), writes=(("ps", 6 + j),))
                if last:
                    ep = epi_n[0] % 2; epi_n[0] += 1
                    for j in range(2):
                        pg.add("dve", lambda v, j=j, ep=ep: v.reciprocal(out=rcp[:, j, 0:qw], in_=self.psum[:, 6 + j, 0:qw]), writes=(("ps", 6 + j), ("rcp", j)))
                        pg.add("dve", lambda v, j=j, ep=ep: v.tensor_tensor(out=tt_[:, j, 0:qw], in0=self.psum[:, 4 + j, 0:qw], in1=rcp[:, j, 0:qw], op=ALU.mult),
                               reads=(("rcp", j),), writes=(("ps", 4 + j), ("tt", j)))
                    pg.add("dve", lambda g, ep=ep: g.scalar_tensor_tensor(out=av[:, ep, 0:qw], in0=tt_[:, 1, 0:qw], scalar=sm[:, 6:7], in1=tt_[:, 0, 0:qw], op0=ALU.mult, op1=ALU.add),
                           reads=(("tt", 0), ("tt", 1), "neglam"), writes=(("av", ep),))
                    pg.add("pool", lambda g, ep=ep: g.tensor_tensor(out=a2[:, ep, 0:qw], in0=av[:, ep, 0:qw], in1=av[:, ep, 0:qw], op=ALU.mult),
                           reads=(("av", ep),), writes=(("a2", ep),))

                    def later(bk, ep=ep, Q0=Q0, h=h):
                        pg.add("pe", lambda t, ep=ep: t.matmul(self.psum[:, bk, 0:qw], self.onesb, a2[:, ep, 0:qw], start=True, stop=True),
                               reads=("onesb", ("a2", ep)), writes=(("ps", bk),))
                        pg.add("act", lambda a, ep=ep: a.activation(out=rs2[:, ep, 0:qw], in_=self.psum[:, bk, 0:qw], func=AF.Sqrt, scale=1.0 / P, bias=self.epsc[:, 0:1]),
                               reads=("epsc",), writes=(("ps", bk), ("rs2", ep)))
                        pg.add("dve", lambda v, ep=ep: v.reciprocal(out=rs2[:, ep, 0:qw], in_=rs2[:, ep, 0:qw]), reads=(("rs2", ep),), writes=(("rs2", ep),))
                        pg.add("dve", lambda v, ep=ep: v.scalar_tensor_tensor(out=ob[:, ep, 0:qw], in0=av[:, ep, 0:qw], scalar=sm[:, 5:6], in1=rs2[:, ep, 0:qw], op0=ALU.mult, op1=ALU.mult),
                               reads=(("av", ep), ("rs2", ep), "sgs"), writes=(("ob", ep),))
                        pg.add("sp", lambda q, ep=ep, Q0=Q0, h=h: q.dma_start(out=self.oT_d[h, :, Q0:Q0 + qw], in_=ob[:, ep, 0:qw]),
                               reads=(("ob", ep),), writes=(("oT_d", h, Q0 // 512),), dma=True)
                    pending.append(later)

            if nwork:
                emit_st(0)
            for wi in range(nwork):
                if wi + 1 < nwork:
                    emit_st(wi + 1)
                had = bool(pending)
                emit_rest(wi)
                if had and not work[wi][3]:
                    flush_pending(2 * (wi % 2))
            flush_pending(2 * ((nwork - 1) % 2))

        for h in range(H):
            do_head(h)
        pg.end_phase("H")

    def attn_out(self, slot, t0):
        c = self.cfg; pg = self.pg; DC = c.DC; T = c.T; H = c.H
        oT = self.hv("oT", 0, [P, DC, T], BF16)
        for h in range(H):
            rk = tuple(("oT_d", h, i) for i in range(t0 // 512, (t0 + T + 511) // 512))
            pg.add("sp", lambda q, h=h: q.dma_start(out=oT[:, h, :], in_=self.oT_d[h, :, t0:t0 + T]), reads=rk, writes=tuple(("oT", h, i) for i in range((T + 511) // 512)), dma=True)
        self.resid_phase(t0, DC, lambda kc, tt: oT[:, kc, tt * P:(tt + 1) * P], lambda kc, tt: (("oT", kc, tt // 4),), self.attn_w_out[slot], 0)
        pg.end_phase("H")

    def hgrn_prep(self, slot, layer):
        c = self.cfg; pg = self.pg; DC = c.DC; L = c.depth
        lbt = self.lbt
        pg.add("sp", lambda q: q.dma_start(out=lbt[:, 0:L, :], in_=self.hgrn_lb_col.rearrange("l p c -> p l c")), writes=("lbt",), dma=True)
        pg.add("sp", lambda q: q.dma_start(out=self.small[:, 8:9], in_=self.hgrn_gcol[slot]), writes=("hgcol",), dma=True)
        pg.add("sp", lambda q: q.dma_start(out=self.identf, in_=self.bdtri_in), reads=("identb",), writes=("identf",), dma=True)
        pg.add("dve", lambda v: v.tensor_copy(out=self.bdtri, in_=self.identf), reads=("identf",), writes=("bdtri",))
        pg.add("sp", lambda q: q.dma_start(out=self.identf, in_=self.ident_in), reads=("bdtri",), writes=("identf",), dma=True)
        pg.add("sp", lambda q: q.dma_start(out=self.scanmask, in_=self.scanmask_in), writes=("scanmask",), dma=True)
        mx = lbt[:, L, :]; den = lbt[:, L + 1, :]; num = lbt[:, L + 2, :]; tmp = lbt[:, L + 3, :]
        lb = lbt[:, L + 4, :]; oml = lbt[:, L + 5, :]
        K = "lbt"
        pg.add("dve", lambda v: v.tensor_copy(out=mx, in_=lbt[:, 0, :]), reads=(K,), writes=(K,))
        for l in range(1, L):
            pg.add("dve", lambda v, l=l: v.tensor_tensor(out=mx, in0=mx, in1=lbt[:, l, :], op=ALU.max), reads=(K,), writes=(K,))
        pg.add("dve", lambda v: v.memset(den, 0.0), reads=(K,), writes=(K,))
        pg.add("dve", lambda v: v.memset(num, 0.0), reads=(K,), writes=(K,))
        for l in range(L):
            pg.add("dve", lambda v, l=l: v.tensor_tensor(out=tmp, in0=lbt[:, l, :], in1=mx, op=ALU.subtract), reads=(K,), writes=(K,))
            pg.add("act", lambda a: a.activation(out=tmp, in_=tmp, func=AF.Exp), reads=(K,), writes=(K,))
            pg.add("dve", lambda v: v.tensor_tensor(out=den, in0=den, in1=tmp, op=ALU.add), reads=(K,), writes=(K,))
            if 1 <= l <= layer:
                pg.add("dve", lambda v: v.tensor_tensor(out=num, in0=num, in1=tmp, op=ALU.add), reads=(K,), writes=(K,))
        pg.add("dve", lambda v: v.reciprocal(out=den, in_=den), reads=(K,), writes=(K,))
        pg.add("dve", lambda v: v.tensor_tensor(out=lb, in0=num, in1=den, op=ALU.mult), reads=(K,), writes=(K,))
        pg.add("dve", lambda v: v.tensor_scalar(out=oml, in0=lb, scalar1=-1.0, scalar2=1.0, op0=ALU.mult, op1=ALU.add), reads=(K,), writes=(K,))
        pg.add("dve", lambda v: v.memset(self.Sst, 0.0), writes=tuple(("Sst", g_) for g_ in range(c.H // 4)))
        pg.add("dve", lambda v: v.memset(self.Sbf, 0.0), writes=tuple(("Sbf", g_) for g_ in range(c.H // 4)))

    def hgrn_mixer(self, slot, t0):
        c = self.cfg; pg = self.pg; DC = c.DC; D = c.D; H = c.H; L = c.depth
        TH = 512
        Win = self.hgrn_w_in[slot]; Wout = self.hgrn_w_out[slot]
        lb = self.lbt[:, L + 4, :]; oml = self.lbt[:, L + 5, :]
        sm = self.small
        o = 0
        qtb = self.hv("qtb", o, [P, 4, TH], BF16); o += 4096
        ktb = self.hv("ktb", o, [P, 4, TH], BF16); o += 4096
        khb = self.hv("khb", o, [P, 1, TH], BF16); o += 1024
        isb = self.hv("isb", o, [P, 4, 512], BF16); o += 4096
        khtm = self.hv("khtm", o, [P, 4, 512], BF16); o += 4096
        gsl = self.hv("gsl", o, [P, 4, TH], BF16); o += 4096
        tmp = self.hv("htmp", o, [P, 4, TH], F32); o += 8192
        smk = self.hv("smk", o, [P, 8, P], BF16); o += 2048
        oacc = self.hv("oacc", o, [P, 4, TH], F32); o += 8192
        o2 = self.hv("o2", o, [P, 1, TH], BF16); o += 1024
        rsb = self.hv("rsb", o, [P, 1, TH], F32); o += 2048
        oTo = self.hv("oTo", o, [P, 4, TH], BF16); o += 4096
        assert o <= self.HBYTES, o
        tA = tmp[:, 0, :]; tB = tmp[:, 1, :]; tC = tmp[:, 2, :]; tD = tmp[:, 3, :]
        NCH = TH // CHUNK
        self.norm_phase(1, t0, T=TH)
        xk = lambda dc: (("xnT", dc, 0, 0), ("xnT", dc, 0, 1))
        banksets = ((0, 1, 2), (3, 6, 7))
        for hg in range(H // 4):
            wq, kq = self.load_w(Win, 0, DC, 0 * D + hg * 512, 512)
            wf, kf = self.load_w(Win, 0, DC, 1 * D + hg * 512, 512)
            wi, ki = self.load_w(Win, 0, DC, 2 * D + hg * 512, 512)
            wg_, kg = self.load_w(Win, 0, DC, 3 * D + hg * 512, 512)

            def prep_head(jj, hg=hg, wq=wq, kq=kq, wf=wf, kf=kf, wg_=wg_, kg=kg):
                h = 4 * hg + jj
                bs = banksets[jj % 2]
                for (wv, wk_, b_) in ((wq, kq, bs[0]), (wf, kf, bs[1]), (wg_, kg, bs[2])):
                    for dc in range(DC):
                        pg.add("pe", lambda t, wv=wv, dc=dc, b_=b_: t.matmul(self.psum[:, b_, :], wv[:, dc, jj * P:(jj + 1) * P], self.xnT[:, dc, 0:TH], start=(dc == 0), stop=(dc == DC - 1)),
                               reads=wk_ + xk(dc), writes=(("ps", b_),))
                pg.add("act", lambda a: a.activation(out=tA, in_=self.psum[:, bs[1], :], func=AF.Sigmoid), writes=(("ps", bs[1]), ("htmp", 0)))
                pg.add("dve", lambda v: v.tensor_scalar(out=tA, in0=tA, scalar1=oml[:, h:h + 1], scalar2=lb[:, h:h + 1], op0=ALU.mult, op1=ALU.add),
                       reads=(("htmp", 0), "lbt"), writes=(("htmp", 0),))
                pg.add("act", lambda a: a.activation(out=tB, in_=tA, func=AF.Ln), reads=(("htmp", 0),), writes=(("htmp", 1),))
                pg.add("dve", lambda v: v.tensor_tensor_scan(out=tC, data0=self.scanmask[:, 0:TH], data1=tB, initial=0.0, op0=ALU.mult, op1=ALU.add),
                       reads=(("htmp", 1), "scanmask"), writes=(("htmp", 2),))
                pg.add("act", lambda a: a.activation(out=tB, in_=tC, func=AF.Exp), reads=(("htmp", 2),), writes=(("htmp", 1),))
                pg.add("act", lambda a: a.activation(out=tD, in_=tC, func=AF.Exp, scale=-1.0), reads=(("htmp", 2),), writes=(("htmp", 3),))
                pg.add("dve", lambda v: v.tensor_tensor(out=qtb[:, jj, :], in0=self.psum[:, bs[0], :], in1=tB, op=ALU.mult),
                       reads=(("htmp", 1),), writes=(("ps", bs[0]), ("qtb", jj)))
                pg.add("pool", lambda g: g.tensor_scalar(out=tA, in0=tA, scalar1=-1.0, scalar2=1.0, op0=ALU.mult, op1=ALU.add), reads=(("htmp", 0),), writes=(("htmp", 0),))
                pg.add("pool", lambda g: g.tensor_tensor(out=tA, in0=tA, in1=tD, op=ALU.mult), reads=(("htmp", 0), ("htmp", 3)), writes=(("htmp", 0),))
                pg.add("act", lambda a: a.copy(out=ktb[:, jj, :], in_=tA), reads=(("htmp", 0),), writes=(("ktb", jj),))
                ebl_b = tB.rearrange("p (c s) -> p c s", s=CHUNK)[:, :, CHUNK - 1:CHUNK]
                pg.add("pool", lambda g: g.tensor_tensor(out=khb[:, 0, :].rearrange("p (c s) -> p c s", s=CHUNK), in0=tA.rearrange("p (c s) -> p c s", s=CHUNK),
                                                         in1=ebl_b.to_broadcast([P, NCH, CHUNK]), op=ALU.mult),
                       reads=(("htmp", 0), ("htmp", 1)), writes=("khb",))
                pg.add("dve", lambda v: v.tensor_copy(out=self.eblt[:, jj, 0:NCH], in_=ebl_b.rearrange("p c s -> p (c s)")), reads=(("htmp", 1),), writes=(("eblt", jj),))
                pg.add("act", lambda a: a.activation(out=gsl[:, jj, :], in_=self.psum[:, bs[2], :], func=AF.Silu), writes=(("ps", bs[2]), ("gsl", jj)))
                tb = 4 + jj % 2
                pbf = self.psum[:, tb, :].bitcast(BF16)
                for tt in range(TH // P):
                    pg.add("pe", lambda t, tt=tt: t.transpose(pbf[:, tt * P:(tt + 1) * P], khb[:, 0, tt * P:(tt + 1) * P], self.identb),
                           reads=("khb", "identb"), writes=(("ps", tb),))
                pg.add("dve", lambda v: v.tensor_copy(out=khtm[:, jj, :], in_=pbf[:, 0:TH]), writes=(("ps", tb), ("khtm", jj)))
            for jj in range(4):
                prep_head(jj)
            for tt in range(TH // P):
                b_ = (0, 1)[tt % 2]
                for dc in range(DC):
                    pg.add("pe", lambda t, tt=tt, dc=dc, b_=b_, wi=wi: t.matmul(self.psum[:, b_, :], self.xnT[:, dc, tt * P:(tt + 1) * P], wi[:, dc, :], start=(dc == 0), stop=(dc == DC - 1)),
                           reads=ki + xk(dc), writes=(("ps", b_),))
                pg.add("act", lambda a, tt=tt, b_=b_: a.copy(out=isb[:, tt, :], in_=self.psum[:, b_, :]), writes=(("ps", b_), ("isb", tt)))
            S4 = self.Sst[:, 4 * hg:4 * hg + 4, :]
            Sb4 = self.Sbf[:, 4 * hg:4 * hg + 4, :]
            skey = ("Sst", hg); sbkey = ("Sbf", hg)
            for tt in range(TH // P):
                par = tt % 2
                sbank = (0, 1)[par]; obank = (2, 3)[par]
                tok = slice(tt * P, (tt + 1) * P)
                for jj in range(4):
                    pg.add("pe", lambda t, jj=jj, tok=tok, sbank=sbank: t.matmul(self.psum[:, sbank, jj * P:(jj + 1) * P], ktb[:, jj, tok], qtb[:, jj, tok], start=(jj == 0), stop=(jj == 3)),
                           reads=(("ktb", jj), ("qtb", jj)), writes=(("ps", sbank),))
                pg.add("dve", lambda v, par=par, sbank=sbank: v.tensor_tensor(out=smk[:, par * 4:par * 4 + 4, :], in0=self.psum[:, sbank, :].rearrange("p (j t) -> p j t", t=P),
                                                                            in1=self.bdtri.unsqueeze(1).to_broadcast([P, 4, P]), op=ALU.mult),
                       reads=("bdtri",), writes=(("ps", sbank), ("smk", par)))
                for cc in range(2):
                    ch = 2 * tt + cc
                    pb = 64 * cc
                    dbank = 4 + cc
                    for jj in range(4):
                        h = 4 * hg + jj
                        pg.add("pe", lambda t, jj=jj, h=h, cc=cc, tt=tt, obank=obank: t.matmul(self.psum[:, obank, jj * P + 64 * cc:jj * P + 64 * cc + 64], self.Sbf[:, h, :], qtb[:, jj, tt * P + 64 * cc:tt * P + 64 * cc + 64],
                                                                                         start=(cc == 0 and jj == 0), stop=False),
                               reads=(sbkey, ("qtb", jj)), writes=(("ps", obank),))
                    for jj in range(4):
                        pg.add("pe", lambda t, jj=jj, pb=pb, tt=tt, dbank=dbank: t.matmul(self.psum[:, dbank, jj * P:(jj + 1) * P], khtm[pb:pb + 64, jj, tt * P:(tt + 1) * P], isb[pb:pb + 64, tt, jj * P:(jj + 1) * P],
                                                                                    start=(jj == 0), stop=(jj == 3)),
                               reads=(("khtm", jj), ("isb", tt)), writes=(("ps", dbank),))
                    pg.add("dve", lambda v, ch=ch, S4=S4: v.tensor_tensor(out=S4, in0=S4, in1=self.eblt[:, :, ch:ch + 1].to_broadcast([P, 4, P]), op=ALU.mult),
                           reads=(skey,) + tuple(("eblt", j_) for j_ in range(4)), writes=(skey,))
                    pg.add("dve", lambda v, dbank=dbank, S4=S4: v.tensor_tensor(out=S4, in0=S4, in1=self.psum[:, dbank, :].rearrange("p (j t) -> p j t", t=P), op=ALU.add),
                           reads=(skey,), writes=(("ps", dbank), skey))
                    pg.add("act", lambda a, S4=S4, Sb4=Sb4: a.copy(out=Sb4, in_=S4), reads=(skey,), writes=(sbkey,))
                for jj in range(4):
                    pg.add("pe", lambda t, jj=jj, tt=tt, par=par, obank=obank: t.matmul(self.psum[:, obank, jj * P:(jj + 1) * P], isb[:, tt, jj * P:(jj + 1) * P], smk[:, par * 4 + jj, :], start=False, stop=(jj == 3)),
                           reads=(("isb", tt), ("smk", par)), writes=(("ps", obank),))
                pg.add("act", lambda a, tok=tok, obank=obank: a.copy(out=oacc[:, :, tok], in_=self.psum[:, obank, :].rearrange("p (j t) -> p j t", t=P)),
                       writes=(("ps", obank), ("oacc", tt)))
            okeys = tuple(("oacc", tt) for tt in range(TH // P))
            for jj in range(4):
                eb_ = 6 + jj % 2
                pg.add("act", lambda a, jj=jj: a.activation(out=o2[:, 0, :], in_=oacc[:, jj, :], func=AF.Square), reads=okeys, writes=("o2",))
                pg.add("pe", lambda t, eb_=eb_: t.matmul(self.psum[:, eb_, :], self.onesb, o2[:, 0, :], start=True, stop=True), reads=("onesb", "o2"), writes=(("ps", eb_),))
                pg.add("act", lambda a, eb_=eb_: a.activation(out=rsb[:, 0, :], in_=self.psum[:, eb_, :], func=AF.Sqrt, scale=1.0 / P, bias=self.epsc[:, 0:1]),
                       reads=("epsc",), writes=(("ps", eb_), "rsb"))
                pg.add("dve", lambda v: v.reciprocal(out=rsb[:, 0, :], in_=rsb[:, 0, :]), reads=("rsb",), writes=("rsb",))
                pg.add("dve", lambda v, jj=jj: v.scalar_tensor_tensor(out=tA, in0=oacc[:, jj, :], scalar=sm[:, 8:9], in1=rsb[:, 0, :], op0=ALU.mult, op1=ALU.mult),
                       reads=okeys + ("rsb", "hgcol"), writes=(("htmp", 0),))
                pg.add("pool", lambda g, jj=jj: g.tensor_tensor(out=oTo[:, jj, :], in0=tA, in1=gsl[:, jj, :], op=ALU.mult), reads=(("htmp", 0), ("gsl", jj)), writes=(("oTo", jj),))
            self.resid_phase(t0, 4, lambda kc, tt: oTo[:, kc, tt * P:(tt + 1) * P], lambda kc, tt: (("oTo", kc),), Wout, 4 * hg, T=TH)
        pg.end_phase("H")


def build_full(cfg):
    b = Builder(cfg)
    c = cfg
    b.consts(); b.copy_x_in()
    slot = {0: 0, 1: 0, 2: 0}
    for layer in range(c.depth):
        kind = c.kinds[layer]
        b.modulation(layer)
        if kind == 0:
            b.attn_prep(slot[0], layer)
        elif kind == 1:
            b.conv_prep(slot[1])
        else:
            b.hgrn_prep(slot[2], layer)
        b.gate_bcast(0, 0.5)
        for t0 in range(0, c.NT, c.T):
            b.ffn(layer, 0, 0, t0)
        b.gate_bcast(1, 1.0)
        if kind == 0:
            for t0 in range(0, c.NT, c.T):
                b.attn_qkv(slot[0], t0)
            b.attn_main(slot[0], layer)
            for t0 in range(0, c.NT, c.T):
                b.attn_out(slot[0], t0)
        elif kind == 1:
            for t0 in range(0, c.NT, c.T):
                b.conv_mixer(slot[1], t0)
        else:
            for t0 in range(0, c.NT, 512):
                b.hgrn_mixer(slot[2], t0)
        slot[kind] += 1
        b.gate_bcast(2, 0.5)
        for t0 in range(0, c.NT, c.T):
            b.ffn(layer, 1, 2, t0)
    b.finish()
    b.pg.emit()
    return b


def _col(v):
    return np.ascontiguousarray(np.asarray(v).reshape(-1, P).T)


def kernel(x, c, ada_w, ada_b, norm_g, ffn_w_gate, ffn_w_up, ffn_w_down,
           attn_w_in, attn_w_out, attn_q_gain, attn_k_gain, attn_lambda, attn_subln_g,
           conv_w_in, conv_w, conv_w_out,
           hgrn_w_in, hgrn_w_out, hgrn_o_norm_g, hgrn_lb_logits):
    from concourse.bass_utils import run_bass_kernel_spmd
    f32 = np.float32
    x = np.asarray(x, f32)
    B, S, D = x.shape
    depth = ada_w.shape[0]
    F = ffn_w_gate.shape[3]
    kinds = [l % 3 for l in range(depth)]
    T = 1024 if S % 1024 == 0 else 512
    cfg = Cfg(D=D, F=F, NT=S, T=T, depth=depth, kinds=kinds, NPFX=0)
    bld = build_full(cfg)
    DC = D // P
    shared = {
        "ada_w": np.asarray(ada_w, f32),
        "ada_b_col": np.stack([_col(ada_b[l]) for l in range(depth)]).astype(f32),
        "normg_col": np.stack([np.concatenate([_col(norm_g[l, s]) for s in range(3)], 1) for l in range(depth)]).astype(f32),
        "ffn_w_gate": np.asarray(ffn_w_gate, f32), "ffn_w_up": np.asarray(ffn_w_up, f32), "ffn_w_down": np.asarray(ffn_w_down, f32),
        "ident": np.eye(P, dtype=f32),
    }
    bones = np.zeros((P, P), f32); bones[:64, :64] = 1; bones[64:, 64:] = 1
    shared["blockones"] = bones
    if 0 in kinds:
        na = attn_w_in.shape[0]
        shared["attn_w_in"] = np.asarray(attn_w_in, f32)
        shared["attn_w_out"] = np.asarray(attn_w_out, f32)
        shared["attn_gcol"] = np.stack([np.stack([np.tile(attn_q_gain[i], 2), np.tile(attn_k_gain[i], 2), attn_subln_g[i], np.zeros(P, f32)], 1) for i in range(na)]).astype(f32)
        shared["attn_lambda"] = np.asarray(attn_lambda, f32).reshape(na, 1, 256)
        bias, mt = host_tables(cfg)
        shared["biastab"] = bias
        shared["mtab"] = mt
        shared["pfx_flag"] = np.zeros((P, 2), f32)
    if 1 in kinds:
        ncv = conv_w_in.shape[0]
        shared["conv_w_in"] = np.asarray(conv_w_in, f32)
        shared["conv_w_out"] = np.asarray(conv_w_out, f32)
        shared["conv_w_col"] = np.stack([np.ascontiguousarray(np.asarray(conv_w[i]).reshape(3, DC, P).transpose(2, 1, 0).reshape(P, DC * 3)) for i in range(ncv)]).astype(f32)
    if 2 in kinds:
        nh = hgrn_w_in.shape[0]
        shared["hgrn_w_in"] = np.asarray(hgrn_w_in, f32)
        shared["hgrn_w_out"] = np.asarray(hgrn_w_out, f32)
        shared["hgrn_gcol"] = np.asarray(hgrn_o_norm_g, f32).reshape(nh, P, 1)
        shared["hgrn_lb_col"] = np.stack([_col(hgrn_lb_logits[l]) for l in range(depth)]).astype(f32)
        pp = np.arange(P)
        shared["bdtri"] = ((pp[:, None] // CHUNK == pp[None, :] // CHUNK) & (pp[:, None] <= pp[None, :])).astype(f32)
        sm_ = np.ones((P, 512), f32); sm_[:, ::CHUNK] = 0
        shared["scanmask"] = sm_
    in_maps = []
    for b in range(B):
        m = dict(shared)
        m["x"] = np.ascontiguousarray(x[b])
        m["c_col"] = _col(np.asarray(c, f32)[b])
        in_maps.append({k: m[k] for k in bld.inputs})
    res = run_bass_kernel_spmd(bld.nc, in_maps, core_ids=list(range(B)))
    out = np.stack([np.asarray(res.results[b]["y"], f32) for b in range(B)])
    return out
```
